# Optimizing a Trainium2 kernel written in Bass

```python
import jax, jax.numpy as jnp
from jax import lax
import numpy as np

D_MODEL = 2048
BATCH = 32
SEQ = 256
DEPTH = 4
DEC_BATCH = 4
DEC_SEQ = 1024
PAST_LEN = 512

GRID_W = 64
H_RET = 8
DK_RET = 128
DV_RET = 128
D_RET = H_RET * DV_RET
H_ATT = 8
H_KV = 2
G_ATT = H_ATT // H_KV
DH_ATT = 128
D_ATT = H_ATT * DH_ATT
D_MIX = D_RET + D_ATT
D_IN = 2 * H_RET * DK_RET + 2 * H_RET * DV_RET + H_ATT * DH_ATT + 2 * H_KV * DH_ATT
WINDOW = 128
BLOCK = 128
D_FF = 5632
CONV_W = 3
N_MOD = 6
ROPE_BASE = 10000.0
EPS = 1e-6
NEG_INF = -1e30

kernel_name = "hybrid_retention_swa_prefix_dit_step"


def rms_norm(x):
    xf = x.astype(jnp.float32)
    return (xf * lax.rsqrt(jnp.mean(xf * xf, axis=-1, keepdims=True) + EPS)).astype(x.dtype)


def modulation(cond, w_mod_l, b_mod_l):
    m = jax.nn.silu(cond) @ w_mod_l + b_mod_l
    return jnp.split(m[..., None, :], N_MOD, axis=-1)


def modulate(x, shift, scale):
    return rms_norm(x) * (1 + scale) + shift


def project(h, w_in_l):
    B, T, _ = h.shape
    sizes = (H_RET * DK_RET, H_RET * DK_RET, H_RET * DV_RET, H_RET * DV_RET,
             H_ATT * DH_ATT, H_KV * DH_ATT, H_KV * DH_ATT)
    cuts = [int(s) for s in np.cumsum(sizes)[:-1]]
    q_r, k_r, v_r, g_r, q_a, k_a, v_a = jnp.split(h @ w_in_l, cuts, axis=-1)
    heads = lambda t, n: t.reshape(B, T, n, -1).transpose(0, 2, 1, 3)
    q_a = q_a.reshape(B, T, H_KV, G_ATT, DH_ATT).transpose(0, 2, 3, 1, 4)
    return (heads(q_r, H_RET), heads(k_r, H_RET), heads(v_r, H_RET), g_r,
            q_a, heads(k_a, H_KV), heads(v_a, H_KV))


def retention_scan(q, k, v, log_gamma, s0):
    B, H, T, DK = q.shape
    DV = v.shape[-1]
    N = T // BLOCK
    q = q.astype(jnp.float32).reshape(B, H, N, BLOCK, DK)
    k = (k.astype(jnp.float32) * DK ** -0.5).reshape(B, H, N, BLOCK, DK)
    v = v.astype(jnp.float32).reshape(B, H, N, BLOCK, DV)
    lg = log_gamma.astype(jnp.float32)
    idx = jnp.arange(BLOCK, dtype=jnp.float32)
    diff = idx[:, None] - idx[None, :]
    intra_decay = jnp.where(diff >= 0, jnp.exp(lg[:, None, None] * jnp.maximum(diff, 0.0)), 0.0)
    scores = jnp.einsum('bhncd,bhnsd->bhncs', q, k) * intra_decay[:, None]
    o_intra = jnp.einsum('bhncs,bhnse->bhnce', scores, v)
    q_dec = jnp.exp(lg[:, None] * (idx + 1.0))
    k_dec = jnp.exp(lg[:, None] * (BLOCK - 1.0 - idx))
    c_dec = jnp.exp(lg * BLOCK)[None, :, None, None]
    kv = jnp.einsum('bhncd,bhnce->nbhde', k * k_dec[:, None, :, None], v)
    qx = jnp.moveaxis(q * q_dec[:, None, :, None], 2, 0)

    def step(s, inp):
        qn, kvn = inp
        return c_dec * s + kvn, jnp.einsum('bhcd,bhde->bhce', qn, s)

    s_fin, o_cross = lax.scan(step, s0.astype(jnp.float32), (qx, kv))
    o = o_intra + jnp.moveaxis(o_cross, 0, 2)
    return o.reshape(B, H, T, DV), s_fin


def bidir_retention(q, k, v, lg_f, lg_b, s0_f, s0_b):
    o_f, s_f = retention_scan(q, k, v, lg_f, s0_f)
    o_b, s_b = retention_scan(jnp.flip(q, 2), jnp.flip(k, 2), jnp.flip(v, 2), lg_b, s0_b)
    return o_f + jnp.flip(o_b, 2), s_f, s_b


def retention_output(o, g, gn_l):
    B, H, T, DV = o.shape
    o = o.transpose(0, 2, 1, 3)
    mu = jnp.mean(o, axis=-1, keepdims=True)
    var = jnp.mean((o - mu) ** 2, axis=-1, keepdims=True)
    on = ((o - mu) * lax.rsqrt(var + EPS)).reshape(B, T, H * DV) * gn_l.astype(jnp.float32)
    return jax.nn.silu(g) * on.astype(g.dtype)


def sink_attend(q_blk, ks, vs, masks, sink):
    scale = DH_ATT ** -0.5
    scores = []
    for k, m in zip(ks, masks):
        s = jnp.einsum('bhgqd,bhkd->bhgqk', q_blk, k).astype(jnp.float32) * scale
        scores.append(s if m is None else jnp.where(m, s, NEG_INF))
    B, _, _, Q, _ = q_blk.shape
    sink_col = jnp.broadcast_to(sink.astype(jnp.float32)[None, :, :, None, None], (B, H_KV, G_ATT, Q, 1))
    p = jax.nn.softmax(jnp.concatenate([sink_col] + scores, axis=-1), axis=-1)
    out = None
    off = 1
    for v, s in zip(vs, scores):
        n = s.shape[-1]
        term = jnp.einsum('bhgqk,bhkd->bhgqd', p[..., off:off + n].astype(v.dtype), v)
        out = term if out is None else out + term
        off += n
    return out


def context_attention(q, k, v, sink):
    B, _, _, S, _ = q.shape

    def blk(i):
        qb = lax.dynamic_slice_in_dim(q, i * BLOCK, BLOCK, axis=3)
        return sink_attend(qb, [k], [v], [None], sink)

    o = lax.map(blk, jnp.arange(S // BLOCK))
    return jnp.moveaxis(o, 0, 3).reshape(B, H_KV, G_ATT, S, DH_ATT)


def latent_attention(q, k, v, ck, cv, sink):
    B, _, _, T, _ = q.shape
    pad = ((0, 0), (0, 0), (WINDOW, WINDOW), (0, 0))
    kp, vp = jnp.pad(k, pad), jnp.pad(v, pad)
    span = BLOCK + 2 * WINDOW
    a = jnp.arange(BLOCK)
    b = jnp.arange(span)
    rel = b[None, :] - a[:, None]

    def blk(i):
        start = i * BLOCK
        qb = lax.dynamic_slice_in_dim(q, start, BLOCK, axis=3)
        kb = lax.dynamic_slice_in_dim(kp, start, span, axis=2)
        vb = lax.dynamic_slice_in_dim(vp, start, span, axis=2)
        key_pos = start - WINDOW + b
        mask = (rel >= 0) & (rel <= 2 * WINDOW) & ((key_pos >= 0) & (key_pos < T))[None, :]
        return sink_attend(qb, [kb, ck], [vb, cv], [mask, None], sink)

    o = lax.map(blk, jnp.arange(T // BLOCK))
    return jnp.moveaxis(o, 0, 3).reshape(B, H_KV, G_ATT, T, DH_ATT)


def axial_angles(T):
    rows = T // GRID_W
    row = jnp.repeat(jnp.arange(rows), GRID_W)
    col = jnp.tile(jnp.arange(GRID_W), rows)
    n = DH_ATT // 4
    inv = ROPE_BASE ** (-jnp.arange(n, dtype=jnp.float32) / n)
    return row[:, None] * inv, col[:, None] * inv


def apply_axial_rope(x, ang_r, ang_c):
    half = x.shape[-1] // 2

    def rot(xh, ang):
        x1, x2 = jnp.split(xh, 2, axis=-1)
        c, s = jnp.cos(ang).astype(x.dtype), jnp.sin(ang).astype(x.dtype)
        return jnp.concatenate([x1 * c - x2 * s, x2 * c + x1 * s], axis=-1)

    return jnp.concatenate([rot(x[..., :half], ang_r), rot(x[..., half:], ang_c)], axis=-1)


def conv_ffn(h, w_up_l, conv_w_l, conv_b_l, w_down_l):
    T = h.shape[1]
    up = jnp.pad(h @ w_up_l, ((0, 0), ((CONV_W - 1) // 2, (CONV_W - 1) // 2), (0, 0)))
    u = conv_b_l + sum(up[:, j:j + T] * conv_w_l[j] for j in range(CONV_W))
    gate, val = jnp.split(u, 2, axis=-1)
    return (jax.nn.silu(gate) * val) @ w_down_l


def trunk_layer(x, mod, w_in_l, w_out_l, log_decay_l, gn_l, sink_l, w_up_l, conv_w_l, conv_b_l,
                w_down_l, ctx_k=None, ctx_v=None, ctx_state=None, rope=None):
    shift1, scale1, gate1, shift2, scale2, gate2 = mod
    B, T, _ = x.shape
    q_r, k_r, v_r, g_r, q_a, k_a, v_a = project(modulate(x, shift1, scale1), w_in_l)
    sink = sink_l.reshape(H_KV, G_ATT)
    if ctx_state is None:
        s0 = jnp.zeros((2, B, H_RET, DK_RET, DV_RET), jnp.float32)
    else:
        s0 = jnp.moveaxis(ctx_state, 1, 0)
    o_r, s_f, s_b = bidir_retention(q_r, k_r, v_r, log_decay_l[0], log_decay_l[1], s0[0], s0[1])
    y_r = retention_output(o_r, g_r, gn_l)
    if ctx_k is None:
        o_a = context_attention(q_a, k_a, v_a, sink)
    else:
        ang_r, ang_c = rope
        q_a = apply_axial_rope(q_a, ang_r, ang_c)
        k_a = apply_axial_rope(k_a, ang_r, ang_c)
        o_a = latent_attention(q_a, k_a, v_a, ctx_k, ctx_v, sink)
    y_a = o_a.transpose(0, 3, 1, 2, 4).reshape(B, T, D_ATT)
    x = x + gate1 * (jnp.concatenate([y_r, y_a], axis=-1) @ w_out_l)
    x = x + gate2 * conv_ffn(modulate(x, shift2, scale2), w_up_l, conv_w_l, conv_b_l, w_down_l)
    return x, k_a, v_a, jnp.stack([s_f, s_b], axis=1)


def setup_inputs(seed: int = 0) -> dict:
    key = jax.random.key(seed)
    ks = jax.random.split(key, 20)
    f32 = jnp.float32
    nrm = lambda k, shape, s: jax.random.normal(k, shape, f32) * s
    base_decay = jnp.log1p(-(2.0 ** (-5.0 - jnp.arange(H_RET, dtype=f32))))
    return {
        "x_prompt": nrm(ks[0], (BATCH, SEQ, D_MODEL), 1.0),
        "x_sample": nrm(ks[1], (DEC_BATCH, DEC_SEQ, D_MODEL), 1.0),
        "cache_k": nrm(ks[2], (DEC_BATCH, DEPTH, H_KV, PAST_LEN, DH_ATT), 1.0),
        "cache_v": nrm(ks[3], (DEC_BATCH, DEPTH, H_KV, PAST_LEN, DH_ATT), 1.0),
        "state_ret": nrm(ks[4], (DEC_BATCH, DEPTH, 2, H_RET, DK_RET, DV_RET), 0.1),
        "c": nrm(ks[5], (DEC_BATCH, D_MODEL), 1.0),
        "c_ctx": nrm(ks[6], (D_MODEL,), 1.0),
        "w_mod": nrm(ks[7], (DEPTH, D_MODEL, N_MOD * D_MODEL), 0.5 * D_MODEL ** -0.5),
        "b_mod": nrm(ks[8], (DEPTH, N_MOD * D_MODEL), 0.02),
        "w_in": nrm(ks[9], (DEPTH, D_MODEL, D_IN), D_MODEL ** -0.5),
        "w_out": nrm(ks[10], (DEPTH, D_MIX, D_MODEL), D_MIX ** -0.5),
        "ret_log_decay": base_decay * (1.0 + nrm(ks[11], (DEPTH, 2, H_RET), 0.05)),
        "ret_gn": 1.0 + nrm(ks[12], (DEPTH, D_RET), 0.02),
        "att_sink": nrm(ks[13], (DEPTH, H_ATT), 0.5),
        "w_up": nrm(ks[14], (DEPTH, D_MODEL, 2 * D_FF), D_MODEL ** -0.5),
        "conv_w": nrm(ks[15], (DEPTH, CONV_W, 2 * D_FF), CONV_W ** -0.5),
        "conv_b": nrm(ks[16], (DEPTH, 2 * D_FF), 0.02),
        "w_down": nrm(ks[17], (DEPTH, D_FF, D_MODEL), D_FF ** -0.5),
        "final_gain": 1.0 + nrm(ks[18], (D_MODEL,), 0.02),
    }


def reference(x_prompt, x_sample, cache_k, cache_v, state_ret, c, c_ctx, w_mod, b_mod, w_in, w_out,
              ret_log_decay, ret_gn, att_sink, w_up, conv_w, conv_b, w_down, final_gain):
    rope = axial_angles(x_sample.shape[1])
    xp, xs = x_prompt, x_sample
    new_k, new_v, new_s = [], [], []
    for l in range(DEPTH):
        weights = (w_in[l], w_out[l], ret_log_decay[l], ret_gn[l], att_sink[l],
                   w_up[l], conv_w[l], conv_b[l], w_down[l])
        xp, k_l, v_l, s_l = trunk_layer(xp, modulation(c_ctx, w_mod[l], b_mod[l]), *weights)
        new_k.append(k_l)
        new_v.append(v_l)
        new_s.append(s_l)
        xs, _, _, _ = trunk_layer(xs, modulation(c, w_mod[l], b_mod[l]), *weights,
                                  ctx_k=cache_k[:, l], ctx_v=cache_v[:, l],
                                  ctx_state=state_ret[:, l], rope=rope)
    y_prompt = rms_norm(xp) * final_gain
    y_sample = rms_norm(xs) * final_gain
    new_cache_k = jnp.stack(new_k, axis=1)
    new_cache_v = jnp.stack(new_v, axis=1)
    new_state_ret = jnp.stack(new_s, axis=1)
    return (y_prompt, y_sample, new_cache_k, new_cache_v, new_state_ret)
```

```python
import numpy as np
import concourse.bass as bass
import concourse.mybir as mybir

F32 = mybir.dt.float32
BF16 = mybir.dt.bfloat16
AF = mybir.ActivationFunctionType
ALU = mybir.AluOpType

ENGS = ("pe", "act", "dve", "pool", "sp")
EPOCH_MAX = 8000


class _Reg:
    __slots__ = ("name", "p0", "p1", "lo", "hi")

    def __init__(self, name, p0, p1, lo, hi):
        self.name, self.p0, self.p1, self.lo, self.hi = name, p0, p1, lo, hi


def _region(ap):
    sp = str(ap.space)
    if "SB" not in sp and "PSUM" not in sp:
        return None
    pat = ap.ap
    row = pat[0][0]
    off = int(ap.offset)
    if row <= 0:
        row = 1 << 40
    p0 = off // row
    lo = off % row
    p1 = p0 + pat[0][1]
    span = 0
    for st, cnt in pat[1:]:
        span += abs(st) * (cnt - 1)
    hi = lo + span + 1
    if "PSUM" in sp:
        lo = (lo // 512) * 512
        hi = ((hi + 511) // 512) * 512
        p0, p1 = 0, 128
    return _Reg(ap.name, p0, p1, lo, hi)


class _Op:
    __slots__ = ("eng", "fn", "reads", "writes", "dma", "deps", "ms", "msval", "sem",
                 "dsem", "dval", "dprev")

    def __init__(self, eng, fn, reads, writes, dma):
        self.eng, self.fn, self.reads, self.writes, self.dma = eng, fn, reads, writes, dma
        self.deps = set()
        self.ms = False
        self.msval = 0
        self.sem = None
        self.dsem = None
        self.dval = 0
        self.dprev = 0


class Prog:
    def __init__(self, nc, n_dma_sems=12):
        self.nc = nc
        self.ops = []
        self.n_dma_sems = n_dma_sems
        self.sync_same_engine = True

    def op(self, eng, fn, reads=(), writes=(), dma=False):
        rr = [r for r in (_region(a) for a in reads if a is not None and not isinstance(a, (int, float))) if r is not None]
        ww = [r for r in (_region(a) for a in writes if a is not None) if r is not None]
        ww = ww + [r for r in rr if r.name.startswith("ps")]
        o = _Op(eng, fn, rr, ww, dma)
        self.ops.append(o)
        return o

    def mm(self, out, lhsT, rhs, start=True, stop=True):
        self.op("pe", lambda e: e.matmul(out, lhsT, rhs, start=start, stop=stop),
                [lhsT, rhs], [out])

    def transpose(self, out, in_, ident):
        self.op("pe", lambda e: e.transpose(out, in_, ident), [in_, ident], [out])

    def act(self, out, in_, func, bias=None, scale=None, accum_out=None, eng="act"):
        kw = {}
        rd = [in_]
        if bias is not None:
            kw["bias"] = bias
            rd.append(bias)
        if scale is not None:
            kw["scale"] = scale
            rd.append(scale)
        wr = [out]
        if accum_out is not None:
            kw["accum_out"] = accum_out
            wr.append(accum_out)
        self.op(eng, lambda e: e.activation(out, in_, func, **kw), rd, wr)

    def tt(self, out, in0, in1, op, eng="dve"):
        self.op(eng, lambda e: e.tensor_tensor(out, in0, in1, op), [in0, in1], [out])

    def ts(self, out, in0, s1, s2, op0, op1=None, eng="dve", accum_out=None):
        rd = [in0, s1, s2]
        wr = [out]
        kw = {}
        if accum_out is not None:
            kw["accum_out"] = accum_out
            wr.append(accum_out)
        if op1 is None:
            self.op(eng, lambda e: e.tensor_scalar(out, in0, s1, s2, op0, **kw), rd, wr)
        else:
            self.op(eng, lambda e: e.tensor_scalar(out, in0, s1, s2, op0, op1, **kw), rd, wr)

    def stt(self, out, in0, scalar, in1, op0, op1, eng="dve"):
        self.op(eng, lambda e: e.scalar_tensor_tensor(out, in0, scalar, in1, op0, op1),
                [in0, scalar, in1], [out])

    def copy(self, out, in_, eng="dve"):
        if eng == "act":
            self.op(eng, lambda e: e.copy(out, in_), [in_], [out])
        else:
            self.op(eng, lambda e: e.tensor_copy(out, in_), [in_], [out])

    def memset(self, ap, val, eng="dve"):
        self.op(eng, lambda e: e.memset(ap, val), [], [ap])

    def recip(self, out, in_):
        self.op("dve", lambda e: e.reciprocal(out, in_), [in_], [out])

    def dma(self, out, in_, eng="sp", **kw):
        self.op(eng, lambda e: e.dma_start(out=out, in_=in_, **kw), [in_], [out], dma=True)

    def _analyze(self):
        recs = {}
        ops = self.ops
        for i, o in enumerate(ops):
            for r in o.reads:
                lst = recs.setdefault(r.name, [])
                keep = []
                for rec in lst:
                    ov = rec[0] < r.p1 and r.p0 < rec[1] and rec[2] < r.hi and r.lo < rec[3]
                    if ov and rec[4] == "w":
                        o.deps.add(rec[5])
                    if (rec[4] == "r" and ops[rec[5]].eng == o.eng and not ops[rec[5]].dma and not o.dma
                            and r.p0 <= rec[0] and rec[1] <= r.p1 and r.lo <= rec[2] and rec[3] <= r.hi):
                        continue
                    keep.append(rec)
                keep.append([r.p0, r.p1, r.lo, r.hi, "r", i])
                recs[r.name] = keep
            for w in o.writes:
                lst = recs.setdefault(w.name, [])
                keep = []
                for rec in lst:
                    ov = rec[0] < w.p1 and w.p0 < rec[1] and rec[2] < w.hi and w.lo < rec[3]
                    if ov and rec[5] != i:
                        o.deps.add(rec[5])
                    if w.p0 <= rec[0] and rec[1] <= w.p1 and w.lo <= rec[2] and rec[3] <= w.hi:
                        continue
                    keep.append(rec)
                keep.append([w.p0, w.p1, w.lo, w.hi, "w", i])
                recs[w.name] = keep
        for i, o in enumerate(ops):
            nd = set()
            for d in o.deps:
                p = ops[d]
                if p.eng == o.eng and not p.dma:
                    if o.eng == "pe" or not self.sync_same_engine:
                        continue
                    if o.dma:
                        nd.add(d)
                        continue
                    nd.add(d)
                    continue
                nd.add(d)
            latest = {}
            nd2 = set()
            for d in nd:
                if ops[d].dma:
                    nd2.add(d)
                else:
                    e_ = ops[d].eng
                    if e_ not in latest or d > latest[e_]:
                        latest[e_] = d
            nd2.update(latest.values())
            o.deps = nd2
            for d in nd2:
                if not ops[d].dma:
                    ops[d].ms = True

    def emit(self, final_eng="sp"):
        nc = self.nc
        self._analyze()
        ops = self.ops
        cnt = {e: 0 for e in ENGS}
        epoch = {e: 0 for e in ENGS}
        n_epochs = {e: 1 for e in ENGS}
        for o in ops:
            if o.ms:
                if cnt[o.eng] >= EPOCH_MAX:
                    epoch[o.eng] += 1
                    cnt[o.eng] = 0
                    n_epochs[o.eng] = epoch[o.eng] + 1
                cnt[o.eng] += 1
                o.sem = (o.eng, epoch[o.eng])
                o.msval = cnt[o.eng]
        dcnt = {}
        drr = {e: 0 for e in ENGS}
        for o in ops:
            if o.dma:
                k = ("D" + o.eng, drr[o.eng] % self.n_dma_sems)
                drr[o.eng] += 1
                o.dprev = dcnt.get(k, 0)
                dcnt[k] = o.dprev + 16
                o.dsem = k
                o.dval = dcnt[k]
        sem_keys = []
        for e in ENGS:
            for ep in range(n_epochs[e]):
                sem_keys.append((e, ep))
        sem_keys += sorted(dcnt.keys())
        self.n_sems = len(sem_keys)
        self.ms_counts = {e: sum(1 for o in ops if o.eng == e and o.ms) for e in ENGS}
        self.op_counts = {e: sum(1 for o in ops if o.eng == e) for e in ENGS}
        final_vals = {}
        for o in ops:
            if o.ms:
                final_vals[o.sem] = max(final_vals.get(o.sem, 0), o.msval)
        for k, v in dcnt.items():
            final_vals[k] = v

        from contextlib import ExitStack
        with ExitStack() as st:
            sems = {}
            for k in sem_keys:
                sems[k] = st.enter_context(nc.semaphore("s_%s_%d" % (k[0], k[1])))
            block = st.enter_context(nc.Block())
            per = {e: [] for e in ENGS}
            for i, o in enumerate(ops):
                per[o.eng].append(i)

            def run_engine(ename, eh):
                waited = {}
                for i in per[ename]:
                    o = ops[i]
                    need = {}
                    for d in o.deps:
                        p = ops[d]
                        if p.dma:
                            k, v = p.dsem, p.dval
                        else:
                            k, v = p.sem, p.msval
                        if waited.get(k, 0) >= v:
                            continue
                        need[k] = max(need.get(k, 0), v)
                    if o.dma and o.dprev > 0:
                        k = o.dsem
                        if waited.get(k, 0) < o.dprev:
                            need[k] = max(need.get(k, 0), o.dprev)
                    for k, v in need.items():
                        eh.wait_ge(sems[k], v)
                        waited[k] = v
                    ins = o.fn(eh)
                    if o.dma:
                        ins.then_inc(sems[o.dsem], 16)
                    elif o.ms:
                        ins.then_inc(sems[o.sem], 1)
                if ename == final_eng:
                    for k, v in final_vals.items():
                        if waited.get(k, 0) < v:
                            eh.wait_ge(sems[k], v)

            @block.tensor
            def _(e):
                run_engine("pe", e)

            @block.scalar
            def _(e):
                run_engine("act", e)

            @block.vector
            def _(e):
                run_engine("dve", e)

            @block.gpsimd
            def _(e):
                run_engine("pool", e)

            @block.sync
            def _(e):
                run_engine("sp", e)


from contextlib import ExitStack
import math

D = 2048
NCH = 16
DEPTH = 4
TOK = 1024
D_IN = 5632
D_FF = 5632
EPS = 1e-6
CFG = {"layers": 4, "groups": ("p", "s"), "nw": 3, "wl": 4, "ncores": 8, "ph": ("mod", "n1", "ret", "att", "wout", "n2", "ffn", "fin")}

_C = {}
def _cdef():
    off = 0
    for name, n in [("ident", 128), ("perm", 128), ("E1", 128), ("E2", 128), ("E3p", 128),
                    ("E3n", 128), ("C1", 128), ("C2", 128), ("N128", 8), ("EF", 2), ("EB", 2),
                    ("lnsc", 1), ("eps", 1)]:
        _C[name] = (off, n)
        off += n
    return off
NCONST = _cdef()
NTAB = 1024 + 1024 + 384


def make_consts():
    c = np.zeros((128, NCONST), np.float32)
    tab = np.zeros((128, NTAB), np.float32)
    _T = {"cos": (0, 1024), "sin": (1024, 1024), "band": (2048, 384)}
    def put(name, arr):
        if name in _T:
            o, n = _T[name]
            tab[:, o:o + n] = arr
            return
        o, n = _C[name]
        c[:, o:o + n] = arr
    sl = np.arange(128, dtype=np.float64)[:, None]
    cl = np.arange(128, dtype=np.float64)[None, :]
    put("ident", np.eye(128))
    perm = np.zeros((128, 128))
    for m in range(128):
        perm[m ^ 32, m] = 1.0
    put("perm", perm)
    put("E1", 128 + cl - sl)
    put("E2", 128 - cl + sl)
    d = cl - sl
    put("E3p", np.where(d >= 0, d, 1e6))
    put("E3n", np.where(d <= 0, -d, 1e6))
    put("C1", np.broadcast_to(cl + 1, (128, 128)))
    put("C2", np.broadcast_to(128 - cl, (128, 128)))
    put("N128", np.broadcast_to(128.0 * np.arange(8)[None, :], (128, 8)))
    ii = np.arange(2)[None, :]
    put("EF", 255 - 128 * ii - sl)
    put("EB", 128 * ii + sl)
    put("lnsc", np.full((128, 1), math.log(128.0 ** -0.5)))
    put("eps", np.full((128, 1), EPS))
    t = np.arange(1024)
    row = (t // 64).astype(np.float64)
    col = (t % 64).astype(np.float64)
    m = np.arange(128)
    inv = 10000.0 ** (-(m % 32).astype(np.float64) / 32.0)
    pos = np.where((m < 64)[:, None], row[None, :], col[None, :])
    ang = pos * inv[:, None]
    put("cos", np.cos(ang))
    sgn = np.where((m % 64) < 32, -1.0, 1.0)[:, None]
    put("sin", np.sin(ang) * sgn)
    kl = np.arange(128)[:, None]
    u = np.arange(384)[None, :]
    put("band", (np.abs(u - 128 - kl) <= 128).astype(np.float64))
    return c, tab

_S = {}
def _sdef():
    off = 0
    for name, n in [("cc", 32), ("bmod", 4 * 96), ("gn", 4 * 8), ("fg", 16),
                    ("lg", 4 * 16), ("sink", 4 * 8)]:
        _S[name] = (off, n)
        off += n
    return off
NSMALL = _sdef()


def make_smalls(c_ctx, c_b, b_mod, conv_w, conv_b, ret_gn, final_gain, ret_log_decay, att_sink):
    s = np.zeros((128, NSMALL), np.float32)
    def put(name, arr):
        o, n = _S[name]
        s[:, o:o + n] = np.asarray(arr, np.float32).reshape(128, n)
    cc = np.stack([c_ctx.reshape(16, 128).T, c_b.reshape(16, 128).T], axis=1)
    put("cc", cc)
    put("bmod", b_mod.reshape(4, 96, 128).transpose(2, 0, 1))
    put("gn", ret_gn.reshape(4, 8, 128).transpose(2, 0, 1))
    put("fg", final_gain.reshape(16, 128).T)
    put("lg", np.broadcast_to(ret_log_decay.reshape(1, 64), (128, 64)))
    put("sink", np.broadcast_to(att_sink.reshape(1, 32), (128, 32)))
    cv = np.concatenate([conv_w, conv_b[:, None, :]], axis=1)
    convs = np.ascontiguousarray(cv.reshape(4, 4, 88, 128).transpose(3, 0, 1, 2).reshape(128, 4, 352), dtype=np.float32)
    return s, convs


def build_program():
    nc = bass.Bass("TRN2", target_bir_lowering=False)
    NL = CFG["layers"]
    groups = CFG["groups"]
    dr = lambda name, shape, kind="ExternalInput": nc.dram_tensor(name, shape, F32, kind=kind).ap()
    xin = {"p": dr("xp_t", [128, NCH, TOK]), "s": dr("xs_t", [128, NCH, TOK])}
    ckd = dr("ck_in", [4, 2, 512, 128])
    cvd = dr("cv_in", [4, 2, 512, 128])
    srd = dr("sr_in", [4, 2, 8, 128, 128])
    smd = dr("smalls", [128, NSMALL])
    cstd = dr("consts", [128, NCONST])
    tabd = dr("tabs", [128, NTAB])
    convd = dr("convs", [128, 4, 352])
    WL = CFG["wl"]
    w_mod = dr("w_mod", [WL, D, 6 * D])
    w_in = dr("w_in", [WL, D, D_IN])
    w_out = dr("w_out", [WL, D, D])
    w_up = dr("w_up", [WL, D, 2 * D_FF])
    w_down = dr("w_down", [WL, D_FF, D])
    yout = {"p": dr("yp_t", [128, NCH, TOK], "ExternalOutput"), "s": dr("ys_t", [128, NCH, TOK], "ExternalOutput")}
    nkd = dr("nk_out", [4, 4, 2, 256, 128], "ExternalOutput")
    nvd = dr("nv_out", [4, 4, 2, 256, 128], "ExternalOutput")
    nsd = dr("ns_out", [4, 4, 2, 8, 128, 128], "ExternalOutput")

    with ExitStack() as st:
        def T(name, shape, dt=F32):
            return st.enter_context(nc.sbuf_tensor(name, shape, dt))
        x = T("x", [128, NCH, TOK])
        hA = T("hA", [128, NCH, TOK], BF16)
        yB = T("yB", [128, NCH, TOK], BF16)
        WS = [T("ws%d" % i, [128, 16, 256], BF16) for i in range(CFG["nw"])]
        R = T("R", [128, 6144], BF16)
        Tf = T("Tf", [128, 4, 512])
        qaT = T("qaT", [128, 2, TOK], BF16)
        PEB = [T("peb%d" % i, [128, 512], BF16) for i in range(2)]
        PMB = [T("pmb%d" % i, [128, 512], BF16) for i in range(2)]
        cst = T("cst", [128, NCONST])
        sm = T("sm", [128, NSMALL])
        convL = T("convL", [128, 352])
        cosT = T("cosT", [128, TOK], BF16)
        sinT = T("sinT", [128, TOK], BF16)
        bandm = T("bandm", [128, 384], BF16)
        ones_bf = T("ones_bf", [128, 128], BF16)
        onesf = T("onesf", [128, 128])
        modv = T("modv", [128, 2, 4, 96])
        scT = T("scT", [128, 32], BF16)
        bases = T("bases", [128, 5, 128])
        scl = T("scl", [128, 2, 8])
        wcol = T("wcol", [128, 2, 2])
        esink = T("esink", [128, 8])
        S0 = T("S0", [128, 2, 128], BF16)
        VS = [T("vs%d" % i, [128, 2, 128], BF16) for i in range(2)]
        STG = [T("stg%d" % i, [128, 2, 128]) for i in range(2)]
        cvS = T("cvS", [128, 2, 4, 128], BF16)
        PS = [st.enter_context(nc.psum_tensor("ps%d" % i, [128, 1024], F32)) for i in range(4)]

        p = Prog(nc)
        rot = {"ps": 0, "ws": 0, "pe": 0, "pm": 0, "vs": 0, "stg": 0}

        def nxt(key, lst):
            t = lst[rot[key] % len(lst)]
            rot[key] += 1
            return t
        ps_next = lambda: nxt("ps", PS[0:3])
        ACC = PS[3]
        rot["acc"] = 0

        def acc_half():
            h_ = rot["acc"] % 2
            rot["acc"] += 1
            return ACC[:, 512 * h_:512 * h_ + 512]
        wslot = lambda: nxt("ws", WS)

        def C(name):
            o, n = _C[name]
            return cst[:, o:o + n]

        def S(name, a=0, b=None):
            o, n = _S[name]
            return sm[:, o + a:o + (n if b is None else b)]

        def wcols(Wl, c0, n):
            return Wl.rearrange("(k p) c -> p k c", p=128)[:, :, c0:c0 + n]

        qT = R[:, 0:1024]
        kT = R[:, 1024:2048]
        sgT = R[:, 2048:3072]
        v_tm = R[:, 3072:4096].rearrange("p (j e) -> p j e", j=8)
        k_tm = R[:, 4096:5120].rearrange("p (j e) -> p j e", j=8)
        qf = R[:, 4096:5120]
        qb = R[:, 5120:6144]
        kaT = R[:, 0:2048].rearrange("p (h t) -> p h t", h=2)
        va_tm = R[:, 2048:4096].rearrange("p (j e) -> p j e", j=8)
        ckT = R[:, 4096:5120].rearrange("p (h t) -> p h t", h=2)

        p.dma(cst[:], cstd, eng="sp")
        p.dma(sm[:], smd, eng="sp")
        p.dma(cosT[:], tabd[:, 0:1024], eng="pool")
        p.dma(sinT[:], tabd[:, 1024:2048], eng="pool")
        p.dma(bandm[:], tabd[:, 2048:2432], eng="pool")
        p.memset(ones_bf[:], 1.0)
        p.memset(onesf[:], 1.0 / 128.0)
        p.act(scT[:], S("cc"), AF.Silu)

        for l in range(NL if "mod" in CFG["ph"] else 0):
            pm_ = ps_next()
            for tl in range(48):
                slot = wslot()
                p.dma(slot[:], wcols(w_mod[l], 256 * tl, 256), eng="pool")
                for mm_ in range(2):
                    mc = 2 * tl + mm_
                    for k in range(16):
                        p.mm(pm_[:, 2 * mc:2 * mc + 2], slot[:, k, 128 * mm_:128 * mm_ + 128],
                             scT[:, k:32:16], start=(k == 0), stop=(k == 15))
            for v in range(2):
                p.tt(modv[:, v, l, :], pm_[:, v:192:2], S("bmod", 96 * l, 96 * l + 96), ALU.add)
                for c0 in (16, 64):
                    p.ts(modv[:, v, l, c0:c0 + 16], modv[:, v, l, c0:c0 + 16], 1.0, None, ALU.add)

        SCALE_A = 128.0 ** -0.5

        def norm_mod(gi, l, sh0, sc0):
            for t in range(2):
                tsl = slice(512 * t, 512 * t + 512)
                ps = ps_next()
                for k in range(16):
                    sq = nxt("pm", PMB)
                    p.act(sq[:], x[:, k, tsl], AF.Square)
                    p.mm(ps[:, 0:512], ones_bf[:], sq[:], start=(k == 0), stop=(k == 15))
                rs = Tf[:, 3, :]
                p.act(rs, ps[:, 0:512], AF.Sqrt, bias=C("eps"), scale=1.0 / D)
                p.recip(rs, rs)
                for k in range(16):
                    tmp = Tf[:, k % 2, :]
                    p.tt(tmp, x[:, k, tsl], rs, ALU.mult)
                    p.act(hA[:, k, tsl], tmp, AF.Identity, bias=modv[:, gi, l, sh0 + k:sh0 + k + 1],
                          scale=modv[:, gi, l, sc0 + k:sc0 + k + 1])

        def proj_fm(slot, c0, rhsbuf, nk, evac):
            ps = ps_next()
            for t in range(2):
                for k in range(nk):
                    p.mm(ps[:, 512 * t:512 * t + 512], slot[:, k, c0:c0 + 128],
                         rhsbuf[:, k, 512 * t:512 * t + 512], start=(k == 0), stop=(k == nk - 1))
            evac(ps)

        def rope_evac(dst):
            def f(ps):
                for t in range(2):
                    tsl = slice(512 * t, 512 * t + 512)
                    xs = Tf[:, 0, :]
                    p.copy(xs, ps[:, tsl], eng="act")
                    pr = ps_next()
                    p.mm(pr[:, 0:512], C("perm"), xs, start=True, stop=True)
                    p.tt(Tf[:, 1, :], xs, cosT[:, tsl], ALU.mult)
                    p.tt(Tf[:, 2, :], pr[:, 0:512], sinT[:, tsl], ALU.mult)
                    p.tt(dst[:, tsl], Tf[:, 1, :], Tf[:, 2, :], ALU.add)
            return f

        def group_norm_out(po, cw, h, l, tok0):
            o_sb = Tf[:, 0, 0:cw]
            osq = Tf[:, 1, 0:cw]
            m_sb = Tf[:, 2, 0:cw]
            msq = Tf[:, 3, 0:cw]
            p.copy(o_sb, po[:, 0:cw], eng="act")
            p.act(osq, po[:, 0:cw], AF.Square)
            pst = ps_next()
            p.mm(pst[:, 0:cw], onesf[:], o_sb, start=True, stop=True)
            p.mm(pst[:, 512:512 + cw], onesf[:], osq, start=True, stop=True)
            p.copy(m_sb, pst[:, 0:cw], eng="act")
            p.tt(msq, m_sb, m_sb, ALU.mult)
            p.tt(msq, pst[:, 512:512 + cw], msq, ALU.subtract)
            p.act(msq, msq, AF.Sqrt, bias=C("eps"))
            p.recip(msq, msq)
            p.tt(o_sb, o_sb, m_sb, ALU.subtract)
            p.tt(o_sb, o_sb, msq, ALU.mult)
            gcol = S("gn", 8 * l + h, 8 * l + h + 1)
            p.stt(yB[:, h, tok0:tok0 + cw], o_sb, gcol, sgT[:, tok0:tok0 + cw], ALU.mult, ALU.mult)

        def retention_head(g, gi, l, h):
            Wl = w_in[l]
            A = wslot()
            p.dma(A[:, :, 0:128], wcols(Wl, 128 * h, 128), eng="pool")
            p.dma(A[:, :, 128:256], wcols(Wl, 1024 + 128 * h, 128), eng="pool")
            B = wslot()
            p.dma(B[:, :, 0:128], wcols(Wl, 3072 + 128 * h, 128), eng="pool")
            p.dma(B[:, :, 128:256], wcols(Wl, 2048 + 128 * h, 128), eng="pool")
            if g == "s":
                p.dma(S0[:], srd[l, :, h].rearrange("r d e -> d r e"), eng="pool")
            proj_fm(A, 0, hA, 16, lambda ps: p.copy(qT, ps[:, 0:1024], eng="act"))
            proj_fm(A, 128, hA, 16, lambda ps: p.copy(kT, ps[:, 0:1024], eng="act"))
            proj_fm(B, 0, hA, 16, lambda ps: p.act(sgT, ps[:, 0:1024], AF.Silu))
            tml = [(v_tm, B)] + ([(k_tm, A)] if g == "p" else [])
            for dst, slot in tml:
                for half in range(2):
                    ps = ps_next()
                    for jj in range(4):
                        j = 4 * half + jj
                        for k in range(16):
                            p.mm(ps[:, 128 * jj:128 * jj + 128], hA[:, k, 128 * j:128 * j + 128],
                                 slot[:, k, 128:256], start=(k == 0), stop=(k == 15))
                    p.copy(dst[:, 4 * half:4 * half + 4, :],
                           ps[:, 0:512].rearrange("p (j e) -> p j e", j=4), eng="dve")
            lgf = S("lg", 16 * l + h, 16 * l + h + 1)
            lgb = S("lg", 16 * l + 8 + h, 16 * l + 8 + h + 1)
            lnsc = C("lnsc")
            p.act(bases[:, 0, :], C("E1"), AF.Exp, bias=lnsc, scale=lgf)
            p.act(bases[:, 1, :], C("E2"), AF.Exp, bias=lnsc, scale=lgb)
            p.act(bases[:, 2, :], C("E3p"), AF.Exp, bias=lnsc, scale=lgf)
            p.act(bases[:, 3, :], C("E3n"), AF.Exp, bias=lnsc, scale=lgb)
            p.tt(bases[:, 2, :], bases[:, 2, :], bases[:, 3, :], ALU.add)
            p.act(scl[:, 0, :], C("N128"), AF.Exp, scale=lgf)
            p.act(scl[:, 1, :], C("N128"), AF.Exp, scale=lgb)
            if g == "s":
                p.act(bases[:, 3, :], C("C1"), AF.Exp, scale=lgf)
                p.act(bases[:, 4, :], C("C2"), AF.Exp, scale=lgb)
                for jj in range(8):
                    bsl = slice(128 * jj, 128 * jj + 128)
                    p.stt(qf[:, bsl], qT[:, bsl], scl[:, 0, jj:jj + 1], bases[:, 3, :], ALU.mult, ALU.mult)
                    p.stt(qb[:, bsl], qT[:, bsl], scl[:, 1, 7 - jj:8 - jj], bases[:, 4, :], ALU.mult, ALU.mult)
            else:
                p.act(wcol[:, 0, :], C("EF"), AF.Exp, bias=lnsc, scale=lgf)
                p.act(wcol[:, 1, :], C("EB"), AF.Exp, bias=lnsc, scale=lgb)
            seqs = [(0, 1024)] if g == "s" else [(256 * s_, 256) for s_ in range(4)]
            for si, (t0, T_) in enumerate(seqs):
                nchunk = T_ // 128
                for cbase in range(0, T_, 512):
                    cw = min(512, T_)
                    po = acc_half()
                    nmm = nchunk + (2 if g == "s" else 0)
                    cnt = 0
                    for i in range(nchunk):
                        pp = ps_next()
                        p.mm(pp[:, 0:cw], kT[:, t0 + 128 * i:t0 + 128 * i + 128],
                             qT[:, t0 + cbase:t0 + cbase + cw], start=True, stop=True)
                        pmb = nxt("pm", PMB)
                        for jj in range(cw // 128):
                            cj = cbase // 128 + jj
                            bsl = slice(128 * jj, 128 * jj + 128)
                            if cj > i:
                                p.stt(pmb[:, bsl], pp[:, bsl], scl[:, 0, cj - i - 1:cj - i], bases[:, 0, :],
                                      ALU.mult, ALU.mult)
                            elif cj < i:
                                p.stt(pmb[:, bsl], pp[:, bsl], scl[:, 1, i - cj - 1:i - cj], bases[:, 1, :],
                                      ALU.mult, ALU.mult)
                            else:
                                p.tt(pmb[:, bsl], pp[:, bsl], bases[:, 2, :], ALU.mult)
                        p.mm(po[:, 0:cw], v_tm[:, t0 // 128 + i, :], pmb[:, 0:cw],
                             start=(cnt == 0), stop=(cnt == nmm - 1))
                        cnt += 1
                    if g == "s":
                        p.mm(po[:, 0:cw], S0[:, 0, :], qf[:, cbase:cbase + cw], start=False, stop=False)
                        p.mm(po[:, 0:cw], S0[:, 1, :], qb[:, cbase:cbase + cw], start=False, stop=True)
                    group_norm_out(po, cw, h, l, t0 + cbase)
                if g == "p":
                    pst = ps_next()
                    for i in range(2):
                        vs = nxt("vs", VS)
                        p.ts(vs[:, 0, :], v_tm[:, 2 * si + i, :], wcol[:, 0, i:i + 1], None, ALU.mult)
                        p.ts(vs[:, 1, :], v_tm[:, 2 * si + i, :], wcol[:, 1, i:i + 1], None, ALU.mult)
                        p.mm(pst[:, 0:256], k_tm[:, 2 * si + i, :], vs[:].rearrange("p r e -> p (r e)"),
                             start=(i == 0), stop=(i == 1))
                    sg_ = nxt("stg", STG)
                    p.copy(sg_[:].rearrange("p r e -> p (r e)"), pst[:, 0:256], eng="act")
                    p.dma(nsd[si, l, :, h].rearrange("r d e -> d r e"), sg_[:], eng="sp")

        def attention(g, gi, l):
            Wl = w_in[l]
            KV = wslot()
            p.dma(KV[:], wcols(Wl, 5120, 256), eng="pool")
            VV = wslot()
            p.dma(VV[:], wcols(Wl, 5376, 256), eng="pool")
            p.act(esink[:], S("sink", 8 * l, 8 * l + 8), AF.Exp)
            for hk in range(2):
                if g == "s":
                    proj_fm(KV, 128 * hk, hA, 16, rope_evac(kaT[:, hk, :]))
                else:
                    proj_fm(KV, 128 * hk, hA, 16, lambda ps, hk=hk: p.copy(kaT[:, hk, :], ps[:, 0:1024], eng="act"))
            Tfv = Tf[:].rearrange("p a (b c) -> p (a b) c", b=2)
            tml = [("v", VV)] + ([("k", KV)] if g == "p" else [])
            for nm, slot in tml:
                for half in range(2):
                    ps = ps_next()
                    for jj in range(4):
                        j = 4 * half + jj
                        for k in range(16):
                            p.mm(ps[:, 256 * jj:256 * jj + 256], hA[:, k, 128 * j:128 * j + 128],
                                 slot[:, k, :], start=(k == 0), stop=(k == 15))
                    psv = ps[:, 0:1024].rearrange("p (j e) -> p j e", j=4)
                    if nm == "v":
                        p.copy(va_tm[:, 4 * half:4 * half + 4, :], psv, eng="dve")
                    if g == "p":
                        p.copy(Tfv[:, 4 * half:4 * half + 4, :], psv, eng="act")
                if g == "p" and "nocdma" not in CFG["ph"]:
                    dst = nvd if nm == "v" else nkd
                    for j in range(8):
                        s_, c_ = j // 2, j % 2
                        p.dma(dst[s_, l, :, 128 * c_:128 * c_ + 128, :].rearrange("h t d -> t h d"),
                              Tfv[:, j, :].rearrange("t (h d) -> t h d", h=2), eng="sp")
            if g == "s":
                for hk in range(2):
                    ks = Tf[:, hk, :].rearrange("p (i d) -> p i d", i=4)
                    p.dma(ks, ckd[l, hk].rearrange("(i p) d -> p i d", p=128), eng="sp")
                    ps = ps_next()
                    for i in range(4):
                        p.transpose(ps[:, 128 * i:128 * i + 128], ks[:, i, :], C("ident"))
                    p.copy(ckT[:, hk, :], ps[:, 0:512], eng="act")
                p.dma(cvS[:], cvd[l].rearrange("h (i p) d -> p h i d", p=128), eng="pool")
            for pq in range(4):
                Q = wslot()
                p.dma(Q[:], wcols(Wl, 4096 + 256 * pq, 256), eng="pool")
                hk = pq // 2
                for m_ in range(2):
                    h = 2 * pq + m_
                    if g == "s":
                        proj_fm(Q, 128 * m_, hA, 16, rope_evac(qaT[:, m_, :]))
                    else:
                        proj_fm(Q, 128 * m_, hA, 16,
                                lambda ps, m_=m_: p.copy(qaT[:, m_, :], ps[:, 0:1024], eng="act"))
                    den = Tf[:, 3, :]
                    if "nocore" in CFG["ph"]:
                        continue
                    if g == "p":
                        for s_ in range(4):
                            t0 = 256 * s_
                            pS = ps_next()
                            for i in range(2):
                                p.mm(pS[:, 256 * i:256 * i + 256], kaT[:, hk, t0 + 128 * i:t0 + 128 * i + 128],
                                     qaT[:, m_, t0:t0 + 256], start=True, stop=True)
                            pe = nxt("pe", PEB)
                            p.act(pe[:, 0:512], pS[:, 0:512], AF.Exp, scale=SCALE_A)
                            for i in range(2):
                                p.mm(pS[:, 512:768], va_tm[:, 2 * s_ + i, 128 * hk:128 * hk + 128],
                                     pe[:, 256 * i:256 * i + 256], start=(i == 0), stop=(i == 1))
                            for i in range(2):
                                p.mm(pS[:, 768:1024], ones_bf[:], pe[:, 256 * i:256 * i + 256],
                                     start=(i == 0), stop=(i == 1))
                            p.ts(den[:, 0:256], pS[:, 768:1024], esink[:, h:h + 1], None, ALU.add)
                            p.recip(den[:, 0:256], den[:, 0:256])
                            p.tt(yB[:, 8 + h, t0:t0 + 256], pS[:, 512:768], den[:, 0:256], ALU.mult)
                    else:
                        for j in range(2):
                            pO = ACC
                            blocks = [("c", i) for i in range(4)] + \
                                     [("l", kc) for kc in range(max(0, 4 * j - 1), min(7, 4 * j + 4) + 1)]
                            for idx, (kind, i) in enumerate(blocks):
                                pS = ps_next()
                                if kind == "c":
                                    qlo, qhi = 512 * j, 512 * j + 512
                                    p.mm(pS[:, 0:512], ckT[:, hk, 128 * i:128 * i + 128], qaT[:, m_, qlo:qhi],
                                         start=True, stop=True)
                                else:
                                    qlo = max(512 * j, 128 * (i - 1))
                                    qhi = min(512 * (j + 1), 128 * (i + 2))
                                    p.mm(pS[:, 0:qhi - qlo], kaT[:, hk, 128 * i:128 * i + 128], qaT[:, m_, qlo:qhi],
                                         start=True, stop=True)
                                n_ = qhi - qlo
                                pe = nxt("pe", PEB)
                                p.act(pe[:, 0:n_], pS[:, 0:n_], AF.Exp, scale=SCALE_A)
                                if kind == "l":
                                    u0 = qlo - (128 * i - 128)
                                    p.tt(pe[:, 0:n_], pe[:, 0:n_], bandm[:, u0:u0 + n_], ALU.mult)
                                    lhsV = va_tm[:, i, 128 * hk:128 * hk + 128]
                                else:
                                    lhsV = cvS[:, hk, i, :]
                                off = qlo - 512 * j
                                last = (idx == len(blocks) - 1)
                                p.mm(pO[:, off:off + n_], lhsV, pe[:, 0:n_], start=(idx == 0), stop=last)
                                p.mm(pO[:, 512 + off:512 + off + n_], ones_bf[:], pe[:, 0:n_],
                                     start=(idx == 0), stop=last)
                            p.ts(den, pO[:, 512:1024], esink[:, h:h + 1], None, ALU.add)
                            p.recip(den, den)
                            p.tt(yB[:, 8 + h, 512 * j:512 * j + 512], pO[:, 0:512], den, ALU.mult)

        def resid_update(slot, nk, gate_col0, gi, l, mt):
            for mm_ in range(2):
                m = 2 * mt + mm_
                ps = ps_next()
                for t in range(2):
                    for k in range(nk):
                        p.mm(ps[:, 512 * t:512 * t + 512], slot[:, k, 128 * mm_:128 * mm_ + 128],
                             yB[:, k, 512 * t:512 * t + 512], start=(k == 0), stop=(k == nk - 1))
                p.stt(x[:, m, :], ps[:, 0:1024], modv[:, gi, l, gate_col0 + m:gate_col0 + m + 1], x[:, m, :],
                      ALU.mult, ALU.add)

        def ffn(g, gi, l):
            segs = [(0, 1024)] if g == "s" else [(256 * s_, 256 * s_ + 256) for s_ in range(4)]
            p.dma(convL[:], convd[:, l, :], eng="sp")
            cwj = lambda j, ch: convL[:, 88 * j + ch:88 * j + ch + 1]
            ug = Tf[:, 0:2, :].rearrange("p a b -> p (a b)")
            uv = Tf[:, 2:4, :].rearrange("p a b -> p (a b)")
            for c0, nsl in ((0, 16), (16, 16), (32, 12)):
                for c in range(c0, c0 + nsl, 2):
                    G = wslot()
                    p.dma(G[:], wcols(w_up[l], 128 * c, 256), eng="pool")
                    V = wslot()
                    p.dma(V[:], wcols(w_up[l], D_FF + 128 * c, 256), eng="pool")
                    for cc in range(2):
                        pg = ps_next()
                        pv = ps_next()
                        for (ps, slot) in ((pg, G), (pv, V)):
                            for t in range(2):
                                for k in range(16):
                                    p.mm(ps[:, 512 * t:512 * t + 512], slot[:, k, 128 * cc:128 * cc + 128],
                                         hA[:, k, 512 * t:512 * t + 512], start=(k == 0), stop=(k == 15))
                        chg = c + cc
                        chv = 44 + c + cc
                        p.act(ug, pg[:, 0:1024], AF.Identity, bias=cwj(3, chg), scale=cwj(1, chg))
                        p.act(uv, pv[:, 0:1024], AF.Identity, bias=cwj(3, chv), scale=cwj(1, chv))
                        for (u_, ps, ch) in ((ug, pg, chg), (uv, pv, chv)):
                            for (a, b) in segs:
                                p.stt(u_[:, a + 1:b], ps[:, a:b - 1], cwj(0, ch), u_[:, a + 1:b], ALU.mult, ALU.add)
                                p.stt(u_[:, a:b - 1], ps[:, a + 1:b], cwj(2, ch), u_[:, a:b - 1], ALU.mult, ALU.add)
                        p.act(ug, ug, AF.Silu)
                        p.tt(yB[:, c - c0 + cc, :], ug, uv, ALU.mult)
                for mt in range(8):
                    Dn = wslot()
                    p.dma(Dn[:, 0:nsl, :],
                          w_down[l][128 * c0:128 * (c0 + nsl), 256 * mt:256 * mt + 256].rearrange("(k p) c -> p k c", p=128),
                          eng="pool")
                    resid_update(Dn, nsl, 80, gi, l, mt)

        for g in groups:
            gi = 0 if g == "p" else 1
            p.dma(x[:], xin[g], eng="sp")
            PH = CFG["ph"]
            for l in range(NL):
                if "n1" in PH:
                    norm_mod(gi, l, 0, 16)
                for h in range(8 if "ret" in PH else 0):
                    retention_head(g, gi, l, h)
                if "att" in PH:
                    attention(g, gi, l)
                for mt in range(8 if "wout" in PH else 0):
                    slot = wslot()
                    p.dma(slot[:], wcols(w_out[l], 256 * mt, 256), eng="pool")
                    resid_update(slot, 16, 32, gi, l, mt)
                if "n2" in PH:
                    norm_mod(gi, l, 48, 64)
                if "ffn" in PH:
                    ffn(g, gi, l)
            for t in range(2 if "fin" in CFG["ph"] else 0):
                tsl = slice(512 * t, 512 * t + 512)
                ps = ps_next()
                for k in range(16):
                    sq = nxt("pm", PMB)
                    p.act(sq[:], x[:, k, tsl], AF.Square)
                    p.mm(ps[:, 0:512], ones_bf[:], sq[:], start=(k == 0), stop=(k == 15))
                rs = Tf[:, 3, :]
                p.act(rs, ps[:, 0:512], AF.Sqrt, bias=C("eps"), scale=1.0 / D)
                p.recip(rs, rs)
                for k in range(16):
                    p.stt(x[:, k, tsl], x[:, k, tsl], S("fg", k, k + 1), rs, ALU.mult, ALU.mult)
            p.dma(yout[g], x[:], eng="sp")
        p.emit()
        build_program.stats = (len(p.ops), p.n_sems, p.ms_counts, p.op_counts)
    return nc


_NC_CACHE = {}


def kernel(x_prompt, x_sample, cache_k, cache_v, state_ret, c, c_ctx, w_mod, b_mod, w_in, w_out,
           ret_log_decay, ret_gn, att_sink, w_up, conv_w, conv_b, w_down, final_gain):
    from concourse.bass_utils import run_bass_kernel_spmd
    f = lambda a: np.ascontiguousarray(np.asarray(a, dtype=np.float32))
    x_prompt, x_sample, cache_k, cache_v, state_ret = map(f, (x_prompt, x_sample, cache_k, cache_v, state_ret))
    w_mod, w_in, w_out, w_up, w_down = map(f, (w_mod, w_in, w_out, w_up, w_down))
    c, c_ctx, b_mod, ret_log_decay, ret_gn, att_sink, conv_w, conv_b, final_gain = map(
        f, (c, c_ctx, b_mod, ret_log_decay, ret_gn, att_sink, conv_w, conv_b, final_gain))
    key = (CFG["layers"], CFG["groups"], CFG["nw"], CFG["ph"], CFG["wl"], CFG["ncores"])
    WL, NCORES = CFG["wl"], CFG["ncores"]
    if WL < 4:
        w_mod, w_in, w_out, w_up, w_down = (np.ascontiguousarray(a[:WL]) for a in (w_mod, w_in, w_out, w_up, w_down))
    if key not in _NC_CACHE:
        _NC_CACHE[key] = build_program()
    nc = _NC_CACHE[key]
    consts, tabs = make_consts()
    to_fm = lambda a: np.ascontiguousarray(a.reshape(TOK, NCH, 128).transpose(2, 1, 0))
    smalls_b = [make_smalls(c_ctx, c[b], b_mod, conv_w, conv_b, ret_gn, final_gain, ret_log_decay, att_sink)
                for b in range(4)]
    in_maps = []
    for core in range(NCORES):
        b = core // 2
        in_maps.append({
            "xp_t": to_fm(x_prompt[4 * core:4 * core + 4]),
            "xs_t": to_fm(x_sample[b]),
            "ck_in": cache_k[b], "cv_in": cache_v[b], "sr_in": state_ret[b],
            "smalls": smalls_b[b][0], "convs": smalls_b[b][1],
            "consts": consts, "tabs": tabs,
            "w_mod": w_mod, "w_in": w_in, "w_out": w_out, "w_up": w_up, "w_down": w_down,
        })
    res = run_bass_kernel_spmd(nc, in_maps, core_ids=list(range(NCORES)))
    r = list(res.results)
    while len(r) < 8:
        r.append({k: np.zeros_like(v) for k, v in r[0].items()})
    from_fm = lambda a: a.transpose(2, 1, 0).reshape(TOK, D)
    y_prompt = np.concatenate([from_fm(r[i]["yp_t"]).reshape(4, 256, D) for i in range(8)], axis=0)
    y_sample = np.stack([from_fm(r[2 * b_]["ys_t"]) for b_ in range(4)], axis=0)
    nk = np.concatenate([r[i]["nk_out"] for i in range(8)], axis=0)
    nv = np.concatenate([r[i]["nv_out"] for i in range(8)], axis=0)
    ns = np.concatenate([r[i]["ns_out"] for i in range(8)], axis=0)
    return (np.ascontiguousarray(y_prompt, dtype=np.float32), np.ascontiguousarray(y_sample, dtype=np.float32),
            np.ascontiguousarray(nk, dtype=np.float32), np.ascontiguousarray(nv, dtype=np.float32),
            np.ascontiguousarray(ns, dtype=np.float32))
```
